# Optimizing a Trainium2 kernel written in Bass

```python
import jax, jax.numpy as jnp
from jax import lax
import numpy as np

D_MODEL = 1024
BATCH = 32
SEQ = 2048
DEPTH = 2

MLA_HEADS = 8
Q_LORA = 384
KV_LORA = 256
QK_NOPE = 64
QK_ROPE = 32
V_HEAD = 64
ROPE_THETA = 10000.0
Q_BLOCK = 128

SG_GROUPS = 8
SG_DIM = 512
SG_CHUNK = 128

RWKV_HEADS = 8
RWKV_HEAD = 64
RWKV_DIM = RWKV_HEADS * RWKV_HEAD
DECAY_LORA = 64
AAA_LORA = 64
GATE_LORA = 128
N_DIR = 2
GN_EPS = 64e-5

N_BRANCH = 3
BRANCH_DIM = 512
D_FF = -(-8 * D_MODEL // (3 * 256)) * 256
NORM_EPS = 1e-6

MLA_IN = Q_LORA + KV_LORA + QK_ROPE
SG_IN = 2 * SG_DIM
RWKV_IN = 3 * RWKV_DIM + N_DIR * DECAY_LORA + N_DIR * AAA_LORA + GATE_LORA
GATE_IN = N_BRANCH * D_MODEL
N_IN = MLA_IN + SG_IN + RWKV_IN + GATE_IN

kernel_name = "hybrid_mla_gmlp_rwkv7_encoder"


def rmsnorm(x, g):
    xf = x.astype(jnp.float32)
    y = xf * lax.rsqrt(jnp.mean(xf * xf, axis=-1, keepdims=True) + NORM_EPS)
    return (y * g.astype(jnp.float32)).astype(x.dtype)


def layernorm(x, g, b, eps=1e-5):
    xf = x.astype(jnp.float32)
    mu = jnp.mean(xf, axis=-1, keepdims=True)
    var = jnp.mean(jnp.square(xf - mu), axis=-1, keepdims=True)
    y = (xf - mu) * lax.rsqrt(var + eps)
    return (y * g.astype(jnp.float32) + b.astype(jnp.float32)).astype(x.dtype)


def rope_angles(positions):
    inv_freq = 1.0 / (ROPE_THETA ** (jnp.arange(0, QK_ROPE, 2, dtype=jnp.float32) / QK_ROPE))
    ang = positions.astype(jnp.float32)[..., None] * inv_freq
    return jnp.cos(ang), jnp.sin(ang)


def apply_rope(x, cos, sin):
    half = QK_ROPE // 2
    xf = x.astype(jnp.float32)
    x1, x2 = xf[..., :half], xf[..., half:]
    return jnp.concatenate([x1 * cos - x2 * sin, x1 * sin + x2 * cos], axis=-1).astype(x.dtype)


def mla_attention(cq, ckv, positions, q_norm_g, w_uq, kv_norm_g, w_ukv):
    B, S, _ = cq.shape
    q = (rmsnorm(cq, q_norm_g) @ w_uq).reshape(B, S, MLA_HEADS, QK_NOPE + QK_ROPE)
    q_nope, q_rope = q[..., :QK_NOPE], q[..., QK_NOPE:]
    c_kv, k_rope = ckv[..., :KV_LORA], ckv[..., KV_LORA:]
    kv = (rmsnorm(c_kv, kv_norm_g) @ w_ukv).reshape(B, S, MLA_HEADS, QK_NOPE + V_HEAD)
    k_nope, v = kv[..., :QK_NOPE], kv[..., QK_NOPE:]
    cos, sin = rope_angles(positions)
    q_rope = apply_rope(q_rope, cos[:, :, None], sin[:, :, None])
    k_rope = apply_rope(k_rope, cos, sin)
    scale = (QK_NOPE + QK_ROPE) ** -0.5
    nb = S // Q_BLOCK

    def to_blocks(t):
        return jnp.swapaxes(t.reshape((B, nb, Q_BLOCK) + t.shape[2:]), 0, 1)

    def attend(blk):
        qn, qr = blk
        s = (jnp.einsum('bqhd,bkhd->bhqk', qn, k_nope)
             + jnp.einsum('bqhd,bkd->bhqk', qr, k_rope))
        p = jax.nn.softmax(s.astype(jnp.float32) * scale, axis=-1).astype(v.dtype)
        return jnp.einsum('bhqk,bkhd->bqhd', p, v)

    o = lax.map(attend, (to_blocks(q_nope), to_blocks(q_rope)))
    return jnp.swapaxes(o, 0, 1).reshape(B, S, MLA_HEADS * V_HEAD)


def spatial_gating(z, ln_g, ln_b, w_s, b_s):
    B, S, _ = z.shape
    z = jax.nn.gelu(z)
    u, v = z[..., :SG_DIM], z[..., SG_DIM:]
    v = layernorm(v, ln_g, ln_b)
    v = v.reshape(B, S // SG_CHUNK, SG_CHUNK, SG_GROUPS, SG_DIM // SG_GROUPS)
    mixed = jnp.einsum('gts,bcsgd->bctgd', w_s, v) + b_s.T[:, :, None]
    return u * mixed.reshape(B, S, SG_DIM)


def centred_delta(z):
    prev = jnp.pad(z[:, :-1], ((0, 0), (1, 0), (0, 0)))
    nxt = jnp.pad(z[:, 1:], ((0, 0), (0, 1), (0, 0)))
    return 0.5 * (prev + nxt) - z


def rwkv7_bidir(z, mu, w0, w2, a0, a2, g2, k_k, k_a, r_k, ln_g, ln_b):
    B, S, _ = z.shape
    f32 = jnp.float32
    z = z + mu * centred_delta(z)
    cuts = np.cumsum([RWKV_DIM, RWKV_DIM, RWKV_DIM, N_DIR * DECAY_LORA, N_DIR * AAA_LORA]).tolist()
    r, k, v, wl, al, gl = jnp.split(z, cuts, axis=-1)
    wl = wl.reshape(B, S, N_DIR, DECAY_LORA)
    al = al.reshape(B, S, N_DIR, AAA_LORA)
    w = (w0 + jnp.einsum('bsnl,nlc->bsnc', jnp.tanh(wl), w2)).astype(f32)
    decay = jnp.exp(-jnp.exp(-jax.nn.softplus(-w) - 0.5))
    a = jax.nn.sigmoid((a0 + jnp.einsum('bsnl,nlc->bsnc', al, a2)).astype(f32))
    g = jax.nn.sigmoid(gl) @ g2

    def heads(t):
        return t.reshape(t.shape[:-1] + (RWKV_HEADS, RWKV_HEAD))

    kk = heads((k * k_k).astype(f32))
    kk = kk / jnp.maximum(jnp.sqrt(jnp.sum(kk * kk, axis=-1, keepdims=True)), 1e-12)
    rf, vf = heads(r.astype(f32)), heads(v.astype(f32))
    k_dir = heads(k.astype(f32)[:, :, None] * (1.0 + (a - 1.0) * k_a.astype(f32)))
    b_dir = kk[:, :, None] * heads(a)
    decay_h = heads(decay)

    def bcast(t):
        return jnp.broadcast_to(t[:, :, None], (B, S, N_DIR) + t.shape[2:])

    def time_major(t):
        t = jnp.stack([t[:, :, 0], jnp.flip(t[:, :, 1], axis=1)], axis=0)
        return jnp.transpose(t, (2, 0, 1, 3, 4))

    xs = (time_major(bcast(rf)), time_major(decay_h), time_major(k_dir),
          time_major(bcast(vf)), time_major(bcast(kk)), time_major(b_dir))

    def step(st, inp):
        r_t, w_t, k_t, v_t, kk_t, b_t = inp
        sa = jnp.einsum('dbhij,dbhj->dbhi', st, kk_t)
        st = (st * w_t[..., None, :] - sa[..., :, None] * b_t[..., None, :]
              + v_t[..., :, None] * k_t[..., None, :])
        return st, jnp.einsum('dbhij,dbhj->dbhi', st, r_t)

    s0 = jnp.zeros((N_DIR, B, RWKV_HEADS, RWKV_HEAD, RWKV_HEAD), f32)
    _, ys = lax.scan(step, s0, xs)
    y = jnp.transpose(ys[:, 0] + jnp.flip(ys[:, 1], axis=0), (1, 0, 2, 3))
    mean = jnp.mean(y, axis=-1, keepdims=True)
    var = jnp.mean(jnp.square(y - mean), axis=-1, keepdims=True)
    y = ((y - mean) * lax.rsqrt(var + GN_EPS)).reshape(B, S, RWKV_DIM)
    y = y * ln_g.astype(f32) + ln_b.astype(f32)
    bonus = jnp.sum(jnp.sum(rf[:, :, None] * k_dir * r_k.astype(f32), axis=-1, keepdims=True), axis=2)
    y = y + (bonus * vf).reshape(B, S, RWKV_DIM)
    return y.astype(z.dtype) * g


def swiglu(h, w_gate, w_up, w_down):
    return (jax.nn.silu(h @ w_gate) * (h @ w_up)) @ w_down


def setup_inputs(seed: int = 0) -> dict:
    key = jax.random.key(seed)
    ks = iter(jax.random.split(key, 40))
    nrm = lambda shape, s: jax.random.normal(next(ks), shape, jnp.float32) * s
    gain = lambda shape: 1.0 + nrm(shape, 0.02)
    L, D = DEPTH, D_MODEL
    x = jax.random.normal(next(ks), (BATCH, SEQ, D), jnp.float32)
    positions = jnp.broadcast_to(jnp.arange(SEQ, dtype=jnp.int32), (BATCH, SEQ))
    return {
        "x": x,
        "positions": positions,
        "attn_norm_g": gain((L, D)),
        "w_in": nrm((L, D, N_IN), D ** -0.5),
        "gate_b": nrm((L, N_BRANCH, D), 0.02),
        "q_norm_g": gain((L, Q_LORA)),
        "w_uq": nrm((L, Q_LORA, MLA_HEADS * (QK_NOPE + QK_ROPE)), Q_LORA ** -0.5),
        "kv_norm_g": gain((L, KV_LORA)),
        "w_ukv": nrm((L, KV_LORA, MLA_HEADS * (QK_NOPE + V_HEAD)), KV_LORA ** -0.5),
        "sg_ln_g": gain((L, SG_DIM)),
        "sg_ln_b": nrm((L, SG_DIM), 0.02),
        "sg_w": nrm((L, SG_GROUPS, SG_CHUNK, SG_CHUNK), SG_CHUNK ** -0.5),
        "sg_b": gain((L, SG_GROUPS, SG_CHUNK)),
        "rw_mu": jax.random.uniform(next(ks), (L, RWKV_IN), jnp.float32),
        "rw_w0": jax.random.uniform(next(ks), (L, N_DIR, RWKV_DIM), jnp.float32, -4.0, 1.0),
        "rw_w2": nrm((L, N_DIR, DECAY_LORA, RWKV_DIM), 0.5 * DECAY_LORA ** -0.5),
        "rw_a0": nrm((L, N_DIR, RWKV_DIM), 0.1),
        "rw_a2": nrm((L, N_DIR, AAA_LORA, RWKV_DIM), 0.5 * AAA_LORA ** -0.5),
        "rw_g2": nrm((L, GATE_LORA, RWKV_DIM), GATE_LORA ** -0.5),
        "rw_k_k": 0.85 + nrm((L, RWKV_DIM), 0.02),
        "rw_k_a": gain((L, RWKV_DIM)),
        "rw_r_k": nrm((L, RWKV_HEADS, RWKV_HEAD), 0.1),
        "rw_ln_g": gain((L, RWKV_DIM)),
        "rw_ln_b": nrm((L, RWKV_DIM), 0.02),
        "w_branch": nrm((L, N_BRANCH, BRANCH_DIM, D), BRANCH_DIM ** -0.5),
        "w_out": nrm((L, D, D), D ** -0.5),
        "ffn_norm_g": gain((L, D)),
        "w_ffn_gate": nrm((L, D, D_FF), D ** -0.5),
        "w_ffn_up": nrm((L, D, D_FF), D ** -0.5),
        "w_ffn_down": nrm((L, D_FF, D), D_FF ** -0.5),
        "final_norm_g": gain((D,)),
    }


def reference(x, positions, attn_norm_g, w_in, gate_b, q_norm_g, w_uq, kv_norm_g, w_ukv,
              sg_ln_g, sg_ln_b, sg_w, sg_b, rw_mu, rw_w0, rw_w2, rw_a0, rw_a2, rw_g2,
              rw_k_k, rw_k_a, rw_r_k, rw_ln_g, rw_ln_b, w_branch, w_out, ffn_norm_g,
              w_ffn_gate, w_ffn_up, w_ffn_down, final_norm_g):
    B, S, D = x.shape
    cuts = np.cumsum([Q_LORA, KV_LORA + QK_ROPE, SG_IN, RWKV_IN]).tolist()
    for l in range(DEPTH):
        h = rmsnorm(x, attn_norm_g[l])
        p = h @ w_in[l]
        p_q, p_kv, p_sg, p_rw, p_gate = jnp.split(p, cuts, axis=-1)
        y_a = mla_attention(p_q, p_kv, positions, q_norm_g[l], w_uq[l], kv_norm_g[l], w_ukv[l])
        y_b = spatial_gating(p_sg, sg_ln_g[l], sg_ln_b[l], sg_w[l], sg_b[l])
        y_c = rwkv7_bidir(p_rw, rw_mu[l], rw_w0[l], rw_w2[l], rw_a0[l], rw_a2[l], rw_g2[l],
                          rw_k_k[l], rw_k_a[l], rw_r_k[l], rw_ln_g[l], rw_ln_b[l])
        branches = jnp.stack([y_a, y_b, y_c], axis=2)
        gates = jax.nn.sigmoid(p_gate.reshape(B, S, N_BRANCH, D) + gate_b[l])
        merged = jnp.sum(gates * jnp.einsum('bsnc,ncd->bsnd', branches, w_branch[l]), axis=2)
        x = x + merged @ w_out[l]
        x = x + swiglu(rmsnorm(x, ffn_norm_g[l]), w_ffn_gate[l], w_ffn_up[l], w_ffn_down[l])
    return rmsnorm(x, final_norm_g)
```

```python
import math
import os as _os
from contextlib import ExitStack

import numpy as np
import concourse.bass as bass
import concourse.mybir as mybir
from concourse.bass_utils import run_bass_kernel_spmd

F32 = mybir.dt.float32
BF16 = mybir.dt.bfloat16
I32 = mybir.dt.int32
AF = mybir.ActivationFunctionType
ALU = mybir.AluOpType

ENGS = ("pe", "act", "dve", "pool", "sp")
ROT = 30000
NDMASEM = 12


class _Stop(Exception):
    pass


class Buf:
    __slots__ = ("t", "lw", "rd", "name", "excl")

    def __init__(self, t, name=""):
        self.t = t
        self.lw = {}
        self.rd = {}
        self.name = name
        self.excl = False

    def __getitem__(self, k):
        return self.t[k]


class FW:
    def __init__(self, nc, stack, self_sync=True):
        self.nc = nc
        self.stacks = [stack]
        self.self_sync = self_sync
        self.lists = {e: [] for e in ENGS}
        self.sems = []
        self.cur = {}
        self.known = {e: {} for e in ENGS}
        self.dmasem = {}
        self.dmarr = {}
        self.ninst = 0
        for e in ENGS:
            self.cur[e] = [self._newsem("e_" + e), 0]
        for q in ("sp", "pool"):
            nslot = NDMASEM if q == "sp" else 3
            self.dmasem[q] = [[self._newsem("d_%s%d" % (q, i)), 0] for i in range(nslot)]
            self.dmarr[q] = 0

    def _newsem(self, name):
        s = self.stacks[0].enter_context(self.nc.semaphore(name + "_%d" % len(self.sems)))
        self.sems.append(s)
        return len(self.sems) - 1

    def sbuf(self, name, shape, dt):
        self.uid = getattr(self, "uid", 0) + 1
        t = self.stacks[-1].enter_context(self.nc.sbuf_tensor("%s_u%d" % (name, self.uid), list(shape), dt))
        return Buf(t, name)

    def psum(self, name, shape, dt):
        t = self.stacks[-1].enter_context(self.nc.psum_tensor("ps_" + name, list(shape), dt))
        b = Buf(t, name)
        b.excl = True
        return b

    def dram(self, name, shape, dt, kind="Internal"):
        t = self.nc.dram_tensor(name, list(shape), dt, kind=kind)
        return Buf(t, name)

    def sub(self, buf, n):
        return [Buf(buf.t, buf.name + "_%d" % i) for i in range(n)]

    def _deps(self, reads, writes):
        deps = {}
        for b in reads:
            for s, v in b.lw.items():
                if deps.get(s, 0) < v:
                    deps[s] = v
            if b.excl:
                for s, v in b.rd.items():
                    if deps.get(s, 0) < v:
                        deps[s] = v
        for b in writes:
            for s, v in b.lw.items():
                if deps.get(s, 0) < v:
                    deps[s] = v
            for s, v in b.rd.items():
                if deps.get(s, 0) < v:
                    deps[s] = v
        return deps

    def _emit_waits(self, eng, deps, skip_own=False):
        own = self.cur[eng][0]
        kn = self.known[eng]
        for s, v in deps.items():
            if s == own and (skip_own or not self.self_sync):
                continue
            if kn.get(s, 0) >= v:
                continue
            kn[s] = v
            self.lists[eng].append(("wait", s, v))
            if hasattr(self, "log"):
                self.log.append((eng, "wait", s, v))

    def _mark(self, reads, writes, tok):
        s, v = tok
        for b in reads:
            if b.rd.get(s, 0) < v:
                b.rd[s] = v
        for b in writes:
            b.lw = {s: v}
            b.rd = {}

    def op(self, eng, fn, reads=(), writes=()):
        if self.ninst >= getattr(self, "maxops", 1 << 60):
            raise _Stop()
        deps = self._deps(reads, writes)
        self._emit_waits(eng, deps, skip_own=(eng == "pe"))
        c = self.cur[eng]
        if c[1] >= ROT:
            c[0] = self._newsem("e_" + eng)
            c[1] = 0
        c[1] += 1
        self.lists[eng].append(("op", fn, c[0], c[1]))
        if hasattr(self, "log"):
            self.log.append((eng, "op", c[0], c[1], [b.name for b in reads], [b.name for b in writes]))
        self._mark(reads, writes, (c[0], c[1]))
        self.ninst += 1

    def dma(self, q, out, in_, reads=(), writes=(), **kw):
        deps = self._deps(reads, writes)
        slot = self.dmasem[q][self.dmarr[q] % len(self.dmasem[q])]
        self.dmarr[q] += 1
        if slot[1] > 0:
            deps[slot[0]] = max(deps.get(slot[0], 0), slot[1])
        self._emit_waits(q, deps)
        if slot[1] >= ROT:
            slot[0] = self._newsem("d_" + q)
            slot[1] = 0
        slot[1] += 16
        sem = self.sems[slot[0]]
        self.lists[q].append(
            ("raw", lambda e, out=out, in_=in_, sem=sem, kw=kw: e.dma_start(out=out, in_=in_, **kw).then_inc(sem, 16)))
        self._mark(reads, writes, (slot[0], slot[1]))
        self.ninst += 1

    def barrier(self):
        deps = {}
        for e in ENGS:
            s, v = self.cur[e]
            if v > 0:
                deps[s] = v
        for q in self.dmasem:
            for s, v in self.dmasem[q]:
                if v > 0:
                    deps[s] = max(deps.get(s, 0), v)
        for e in ENGS:
            own = self.cur[e][0]
            kn = self.known[e]
            for s, v in deps.items():
                if s == own or kn.get(s, 0) >= v:
                    continue
                kn[s] = v
                self.lists[e].append(("wait", s, v))

    def scope(self):
        fw = self

        class _Sc:
            def __enter__(s_):
                s_.st = ExitStack()
                s_.st.__enter__()
                fw.stacks.append(s_.st)

            def __exit__(s_, *a):
                fw.barrier()
                fw.stacks.pop()
                s_.st.__exit__(None, None, None)
                return False
        return _Sc()

    def finish(self):
        self.barrier()
        nc = self.nc
        L = self.lists
        engsems = set()
        for e in ENGS:
            for it in L[e]:
                if it[0] == "op":
                    engsems.add(it[2])
        needed = {}
        for e in ENGS:
            for it in L[e]:
                if it[0] == "wait" and it[1] in engsems:
                    needed.setdefault(it[1], set()).add(it[2])
        phys = {}
        for s, vs in needed.items():
            for i, v in enumerate(sorted(vs)):
                phys[(s, v)] = i + 1
        sems = self.sems

        def replay(e, items):
            for it in items:
                if it[0] == "wait":
                    _, s, v = it
                    e.wait_ge(sems[s], phys[(s, v)] if s in engsems else v)
                elif it[0] == "op":
                    _, fn, s, c = it
                    ins = fn(e)
                    if (s, c) in phys:
                        ins.then_inc(sems[s], 1)
                else:
                    it[1](e)

        with nc.Block() as block:
            @block.tensor
            def _(e):
                replay(e, L["pe"])

            @block.scalar
            def _(e):
                replay(e, L["act"])

            @block.vector
            def _(e):
                replay(e, L["dve"])

            @block.gpsimd
            def _(e):
                replay(e, L["pool"])

            @block.sync
            def _(e):
                replay(e, L["sp"])


D = 1024
S = 2048
KC = 8
NTG = 4
TG = 512
NTT = 16
NCOLS = 96
CW = 128 + 128 + 1024 + 512 + 2
DFF = 2816
NF = 22
SCALE = 96 ** -0.5
C_SIG = 2.0 * math.sqrt(2.0 / math.pi)
EM05 = math.exp(-0.5)


def build(NSEQ=4, NL=2, dbg=False, stop=99):
    nc = bass.Bass("TRN2", target_bir_lowering=False)

    def stage(n):
        if n > stop:
            raise _Stop()

    NTOK = NSEQ * S
    with ExitStack() as st:
        fw = FW(nc, st, self_sync=not bool(_os.environ.get("NOSELF")))
        import os as _os2
        if _os2.environ.get("MAXOPS"):
            fw.maxops = int(_os2.environ["MAXOPS"])
        EI = "ExternalInput"
        x_in = fw.dram("x", [NTOK, D], F32, EI)
        pos_in = fw.dram("pos", [NSEQ, S], I32, EI)
        cols_in = fw.dram("cols", [NL, 128, NCOLS], F32, EI)
        fing_in = fw.dram("fing", [128, 8], F32, EI)
        cst_in = fw.dram("cst", [128, CW], F32, EI)
        sgb_in = fw.dram("sgb", [NL, 128, 4, 128], F32, EI)
        sgln_in = fw.dram("sgln", [NL, 2, 512], F32, EI)
        sgw_in = fw.dram("sgwT", [NL, 128, 8 * 128], F32, EI)
        w_in = fw.dram("w_in", [NL, D, 6688], F32, EI)
        w_krsw = fw.dram("w_krsw", [NL, D, 32], F32, EI)
        w_uq = fw.dram("w_uq", [NL, 384, 768], F32, EI)
        w_uqsw = fw.dram("w_uqsw", [NL, 384, 256], F32, EI)
        w_ukv = fw.dram("w_ukv", [NL, 256, 1024], F32, EI)
        rw_w2 = fw.dram("rw_w2", [NL, 128, 512], F32, EI)
        rw_a2 = fw.dram("rw_a2", [NL, 128, 512], F32, EI)
        rw_g2 = fw.dram("rw_g2", [NL, 128, 512], F32, EI)
        w_br = fw.dram("w_branch", [NL, 3, 512, D], F32, EI)
        w_out = fw.dram("w_out", [NL, D, D], F32, EI)
        w_fg = fw.dram("w_ffn_gate", [NL, D, DFF], F32, EI)
        w_fu = fw.dram("w_ffn_up", [NL, D, DFF], F32, EI)
        w_fd = fw.dram("w_ffn_down", [NL, DFF, D], F32, EI)
        out_d = fw.dram("out", [NTOK, D], F32, "ExternalOutput")
        dk = "ExternalOutput" if dbg else "Internal"
        xres = fw.dram("xres", [D, NTOK], F32, dk)
        ymix = fw.dram("ymix", [3, 512, S], BF16, dk)

        def wt(name, kc, ntile):
            return fw.dram(name, [ntile, 128, kc, 128], BF16)
        Wt = []
        for l in range(NL):
            Wt.append(dict(
                mla=wt("wt_mla%d" % l, 8, 5), kr=fw.dram("wt_kr%d" % l, [128, 8, 64], BF16),
                sgu=wt("wt_sgu%d" % l, 8, 4), sgv=fw.dram("wt_sgv%d" % l, [128, 8, 512], BF16),
                rw=wt("wt_rw%d" % l, 8, 15), gt=wt("wt_gt%d" % l, 8, 24),
                uq=fw.dram("wt_uq%d" % l, [128, 3, 768], BF16), uqsw=fw.dram("wt_uqsw%d" % l, [128, 3, 256], BF16),
                ukv=fw.dram("wt_ukv%d" % l, [128, 2, 1024], BF16),
                br=wt("wt_br%d" % l, 4, 24), wo=wt("wt_wo%d" % l, 8, 8),
                fg=wt("wt_fg%d" % l, 8, NF), fu=wt("wt_fu%d" % l, 8, NF), fd=wt("wt_fd%d" % l, NF, 8),
                sm=fw.dram("wt_sm%d" % l, [128, 3, 512], BF16), sgw=fw.dram("wt_sgw%d" % l, [128, 1024], BF16),
            ))

        def cast_tiles(dst, src2d, kc_n, ntile):
            for kc in range(kc_n):
                s_ap = src2d[kc * 128:(kc + 1) * 128, :].rearrange("p (t j) -> p t j", j=128)
                d_ap = dst[:, :, kc, :].rearrange("t p j -> p t j")
                fw.dma("pool", d_ap, s_ap, reads=[], writes=[Buf(None)])

        def cast_rows(dst3, src2d, kc_n):
            for kc in range(kc_n):
                fw.dma("pool", dst3[:, kc, :], src2d[kc * 128:(kc + 1) * 128, :], reads=[], writes=[Buf(None)])

        for l in range(NL):
            W = Wt[l]
            cast_tiles(W["mla"], w_in[l, :, 0:640], 8, 5)
            cast_rows(W["kr"][:, :, 0:32], w_in[l, :, 640:672], 8)
            cast_rows(W["kr"][:, :, 32:64], w_krsw[l, :, :], 8)
            cast_tiles(W["sgu"], w_in[l, :, 672:1184], 8, 4)
            cast_rows(W["sgv"], w_in[l, :, 1184:1696], 8)
            cast_tiles(W["rw"], w_in[l, :, 1696:3616], 8, 15)
            cast_tiles(W["gt"], w_in[l, :, 3616:6688], 8, 24)
            cast_rows(W["uq"], w_uq[l, :, :], 3)
            cast_rows(W["uqsw"], w_uqsw[l, :, :], 3)
            cast_rows(W["ukv"], w_ukv[l, :, :], 2)
            for n in range(3):
                cast_tiles(W["br"][n * 8:(n + 1) * 8], w_br[l, n, :, :], 4, 8)
            cast_tiles(W["wo"], w_out[l, :, :], 8, 8)
            cast_tiles(W["fg"], w_fg[l, :, :], 8, NF)
            cast_tiles(W["fu"], w_fu[l, :, :], 8, NF)
            cast_tiles(W["fd"], w_fd[l, :, :], NF, 8)
            fw.dma("pool", W["sm"][:, 0, :], rw_w2[l, :, :], reads=[], writes=[Buf(None)])
            fw.dma("pool", W["sm"][:, 1, :], rw_a2[l, :, :], reads=[], writes=[Buf(None)])
            fw.dma("pool", W["sm"][:, 2, :], rw_g2[l, :, :], reads=[], writes=[Buf(None)])
            fw.dma("pool", W["sgw"][:, :], sgw_in[l, :, :], reads=[], writes=[Buf(None)])

        cst = fw.sbuf("cst", [128, CW], F32)
        fw.dma("sp", cst[:], cst_in[:], writes=[cst])
        identb = fw.sbuf("identb", [128, 128], BF16)
        onesb = fw.sbuf("onesb", [128, 128], BF16)
        blkb = fw.sbuf("blkb", [128, 128], BF16)
        identq = fw.sbuf("identq", [128, 4, 128], BF16)
        fing = fw.sbuf("fing", [128, 8], F32)
        kcol = fw.sbuf("kcol", [128, 8], F32)
        cmask = fw.sbuf("cmask", [128, S], BF16)
        fw.dma("sp", fing[:], fing_in[:], writes=[fing])
        fw.op("dve", lambda e: e.tensor_copy(out=identb[:], in_=cst[:, 0:128]), [cst], [identb])
        fw.op("dve", lambda e: e.tensor_copy(out=blkb[:], in_=cst[:, 128:256]), [cst], [blkb])
        fw.op("dve", lambda e: e.memset(onesb[:], 1.0), [], [onesb])
        for q in range(4):
            fw.op("dve", lambda e, q=q: e.tensor_copy(out=identq[:, q, :], in_=cst[:, 0:128]), [cst], [identq])
        KV = [1e-6, 1e-5, 64e-5, 0.25, 1.0, 0.0, 1e-24, 0.5]
        for i, v in enumerate(KV):
            fw.op("dve", lambda e, i=i, v=v: e.memset(kcol[:, i:i + 1], v), [kcol], [kcol])
        fw.op("pool", lambda e: e.memset(cmask[:], 1.0), [], [cmask])
        fw.op("pool", lambda e: e.memset(cmask[:].rearrange("p (c t) -> p c t", t=128)[:, :, 0:1], 0.0), [cmask], [cmask])
        identf = cst[:, 0:128]
        mAB = cst[:, 256:1280].rearrange("p (q f) -> p q f", q=4)
        mC = cst[:, 1280:1792].rearrange("p (q f) -> p q f", q=4)
        EPS6, EPS5, EPSG, QTR, ONE, ZERO, TINY, HALF = [kcol[:, i:i + 1] for i in range(8)]

        pb = [fw.psum("pb%d" % i, [128, 512], F32) for i in range(7)]
        pbh = fw.psum("pbh", [128, 1024], BF16)
        psr = [0]

        def ps_next(k=6):
            b = pb[psr[0] % k]
            psr[0] += 1
            return b

        fw.barrier()

        def mm(out, lhsT, rhs, st_, sp_, reads, writes):
            fw.op("pe", lambda e: e.matmul(out, lhsT=lhsT, rhs=rhs, start=st_, stop=sp_), reads, writes)

        def tr(out, in_, ident, reads, writes):
            fw.op("pe", lambda e: e.transpose(out, in_, ident), reads, writes)

        def act(out, in_, func, reads, writes, bias=None, scale=1.0):
            if bias is None:
                fw.op("act", lambda e: e.activation(out=out, in_=in_, func=func, scale=scale), reads, writes)
            else:
                fw.op("act", lambda e: e.activation(out=out, in_=in_, func=func, bias=bias, scale=scale), reads, writes)

        def cp(eng, out, in_, reads, writes):
            if eng == "act":
                fw.op("act", lambda e: e.copy(out=out, in_=in_), reads, writes)
            else:
                fw.op(eng, lambda e: e.tensor_copy(out=out, in_=in_), reads, writes)

        def tt(eng, out, in0, in1, op, reads, writes):
            fw.op(eng, lambda e: e.tensor_tensor(out=out, in0=in0, in1=in1, op=op), reads, writes)

        def ts(eng, out, in0, s1, s2, op0, op1, reads, writes):
            if op1 is None:
                fw.op(eng, lambda e: e.tensor_scalar(out=out, in0=in0, scalar1=s1, scalar2=None, op0=op0), reads, writes)
            else:
                fw.op(eng, lambda e: e.tensor_scalar(out=out, in0=in0, scalar1=s1, scalar2=s2, op0=op0, op1=op1), reads, writes)

        def stt(out, in0, scalar, in1, op0, op1, reads, writes):
            fw.op("dve", lambda e: e.scalar_tensor_tensor(out=out, in0=in0, scalar=scalar, in1=in1, op0=op0, op1=op1), reads, writes)

        def recip(out, in_, reads, writes):
            fw.op("dve", lambda e: e.reciprocal(out=out, in_=in_), reads, writes)

        class Ring:
            def __init__(self, name, shape, dt, n):
                self.b = [fw.sbuf("%s%d" % (name, i), shape, dt) for i in range(n)]
                self.i = 0

            def get(self):
                b = self.b[self.i % len(self.b)]
                self.i += 1
                return b

        def xres_view(t0, n):
            return xres[:, :].rearrange("(c p) t -> p c t", p=128)[:, :, t0:t0 + n]

        def rms_feature(xt, xtb, nchunk, gcols, inv_n, eps, out_fn, sq, rs, width, gb=None):
            act(sq[:, 0:nchunk, 0:width], xt, AF.Square, [xtb], [sq])
            ps = ps_next()
            for c in range(nchunk):
                mm(ps[:, 0:width], onesb[:], sq[:, c, 0:width], c == 0, c == nchunk - 1, [sq, onesb], [ps])
            act(rs[:, 0:width], ps[:, 0:width], AF.Sqrt, [ps, kcol], [rs], bias=eps, scale=inv_n)
            recip(rs[:, 0:width], rs[:, 0:width], [rs], [rs])
            for c in range(nchunk):
                o, wr = out_fn(c)
                stt(o, xt[:, c, :] if nchunk > 1 or True else xt, gcols[:, c:c + 1], rs[:, 0:width], ALU.mult, ALU.mult,
                    [xtb, rs] + ([gb] if gb is not None else []), wr)

        with fw.scope():
            if stop < 1:
                NTOK_P = 0
            else:
                NTOK_P = NTOK
            xin = Ring("xin", [128, D], F32, 2)
            xo = Ring("xo", [128, 8, 128], F32, 2)
            for tt_ in range(NTOK_P // 128):
                xt = xin.get()
                fw.dma("sp", xt[:], x_in[tt_ * 128:(tt_ + 1) * 128, :], writes=[xt])
                o = xo.get()
                for half in range(2):
                    ps = ps_next()
                    for c4 in range(4):
                        c = half * 4 + c4
                        tr(ps[:, c4 * 128:(c4 + 1) * 128], xt[:, c * 128:(c + 1) * 128], identf, [xt, cst], [ps])
                    cp("act" if half else "dve", o[:, half * 4:(half + 1) * 4, :],
                       ps[:, :].rearrange("p (c t) -> p c t", t=128), [ps], [o])
                fw.dma("sp", xres_view(tt_ * 128, 128), o[:], reads=[o], writes=[xres])

        def phase_rwkv(s, l, W, colt, hT, hTs, tgs):
            with fw.scope():
                wsm = fw.sbuf("wsm", [128, 3, 512], BF16)
                fw.dma("sp", wsm[:], W["sm"][:, :, :], reads=[WB], writes=[wsm])
                omm = fw.sbuf("omm", [128, 15], F32)
                hmu = fw.sbuf("hmu", [128, 15], F32)
                omka = fw.sbuf("omka", [128, 4], F32)
                ts("dve", omm[:, :], colt[:, 45:60], -1.0, 1.0, ALU.mult, ALU.add, [colt], [omm])
                ts("dve", hmu[:, :], colt[:, 45:60], 0.5, None, ALU.mult, None, [colt], [hmu])
                ts("dve", omka[:, :], colt[:, 80:84], -1.0, 1.0, ALU.mult, ALU.add, [colt], [omka])
                zr = fw.sbuf("zr", [128, S + 2], F32)
                fw.op("pool", lambda e: e.memset(zr[:, 0:1], 0.0), [], [zr])
                fw.op("pool", lambda e: e.memset(zr[:, S + 1:S + 2], 0.0), [zr], [zr])
                nb = fw.sbuf("nb", [128, S], F32)
                wrw = Ring("wrw", [128, 8, 128], BF16, 2)

                def proj_chunk(n, out_ap, out_buf):
                    wt_ = wrw.get()
                    fw.dma("sp", wt_[:], W["rw"][n, :, :, :], reads=[WB], writes=[wt_])
                    for tg in range(NTG):
                        ps = ps_next()
                        for c in range(8):
                            mm(ps[:, :], wt_[:, c, :], hT[:, c, tgs(tg)], c == 0, c == 7, [wt_, hTs[tg]], [ps])
                        cp("act", zr[:, 1 + tg * TG:1 + (tg + 1) * TG], ps[:, :], [ps], [zr])
                    tt("pool", nb[:, :], zr[:, 0:S], zr[:, 2:S + 2], ALU.add, [zr], [nb])
                    ts("pool", nb[:, :], nb[:, :], hmu[:, n:n + 1], None, ALU.mult, None, [nb, hmu], [nb])
                    stt(out_ap, zr[:, 1:S + 1], omm[:, n:n + 1], nb[:, :], ALU.mult, ALU.add, [zr, omm, nb], [out_buf])

                rT = fw.sbuf("rT", [128, S], F32)
                kT = fw.sbuf("kT", [128, S], F32)
                vT = fw.sbuf("vT", [128, S], BF16)
                wlT = fw.sbuf("wlT", [128, S], BF16)
                alT = fw.sbuf("alT", [128, S], BF16)
                glT = fw.sbuf("glT", [128, S], BF16)
                proj_chunk(12, rT[:, :], rT)
                act(wlT[:, :], rT[:, :], AF.Tanh, [rT], [wlT])
                proj_chunk(13, alT[:, :], alT)
                proj_chunk(14, rT[:, :], rT)
                act(glT[:, :], rT[:, :], AF.Sigmoid, [rT], [glT])

                stage(5.1)
                Vtm = fw.sbuf("Vtm", [128, 16, 128], BF16)
                AR = [fw.sbuf("AR%d" % d, [128, 16, 2, 128], BF16) for d in range(2)]
                KTl = [fw.sbuf("KTl%d" % d, [128, S], BF16) for d in range(2)]
                BTl = [fw.sbuf("BTl%d" % d, [128, S], BF16) for d in range(2)]
                KBh = [fw.sbuf("KBh%d" % d, [128, 16, 2, 128], BF16) for d in range(2)]
                pC = [fw.sbuf("pC%d" % d, [128, 16], F32) for d in range(2)]
                bonusv = fw.sbuf("bonusv", [128, S], F32)
                tF = {k: Ring("rt_" + k, [128, TG], F32, 1) for k in ("kkn", "rn", "lw", "inc", "E", "ad", "kd", "bd", "kds")}
                tH = {k: Ring("rh_" + k, [128, TG], BF16, 1) for k in ("sqb", "btb")}
                KBfr = Ring("KBf", [128, 2, TG], BF16, 2)
                totr = Ring("totc", [128, 4], F32, 2)
                LM_A = fw.sbuf("LM_A", [128, 4, 256], BF16)
                LM_B = fw.sbuf("LM_B", [128, 4, 256], BF16)
                Pq = [fw.sbuf("Pq%d" % i, [128, 4, 128], BF16) for i in range(2)]
                PTq = [fw.sbuf("PTq%d" % i, [128, 4, 128], BF16) for i in range(2)]
                IPT = fw.sbuf("IPT", [128, 7, 4, 128], BF16)
                IPTs = fw.sub(IPT, 7)
                Zq = [fw.sbuf("Zq%d" % i, [128, 4, 128], BF16) for i in range(2)]
                Ap = fw.sbuf("Ap", [128, 4, 64], BF16)
                UV = fw.sbuf("UV", [128, 4, 64], F32)
                ApT = fw.sbuf("ApT", [128, 2, 128], BF16)
                Uq = fw.sbuf("Uq", [128, 4, 64], BF16)
                Hf = fw.sbuf("Hf", [128, 2, 64], F32)
                Hb = fw.sbuf("Hb", [128, 2, 64], BF16)
                eT = {"mean": tF["lw"], "msq": tF["inc"], "var": tF["E"], "t": tF["ad"]}
                ycr = Ring("yc", [128, TG], BF16, 2)

                def v3(ap):
                    return ap.rearrange("p (c t) -> p c t", t=128)

                for cc in range(4):
                    ccs = slice(cc * 128, (cc + 1) * 128)
                    proj_chunk(cc, rT[:, :], rT)
                    proj_chunk(4 + cc, kT[:, :], kT)
                    proj_chunk(8 + cc, vT[:, :], vT)
                    for half in range(2):
                        for j in range(8):
                            tile = half * 8 + j
                            tr(pbh[:, j * 128:(j + 1) * 128], vT[:, tile * 128:(tile + 1) * 128], identb[:], [vT, identb], [pbh])
                        cp("act", Vtm[:, half * 8:(half + 1) * 8, :], pbh[:, :].rearrange("p (c t) -> p c t", t=128), [pbh], [Vtm])
                    stage(5.2)
                    for tg in range(NTG):
                        cols = tgs(tg)
                        c0 = tg * 4
                        kkn = tF["kkn"].get()
                        rn = tF["rn"].get()
                        sqb = tH["sqb"].get()
                        ts("dve", kkn[:, :], kT[:, cols], colt[:, 76 + cc:77 + cc], None, ALU.mult, None, [kT, colt], [kkn])
                        act(sqb[:, :], kkn[:, :], AF.Square, [kkn], [sqb])
                        ps = ps_next()
                        mm(ps[:, :], blkb[:, :], sqb[:, :], True, True, [blkb, sqb], [ps])
                        act(rn[:, :], ps[:, :], AF.Sqrt, [ps], [rn])
                        ts("dve", rn[:, :], rn[:, :], 1e-12, None, ALU.max, None, [rn], [rn])
                        recip(rn[:, :], rn[:, :], [rn], [rn])
                        tt("pool", kkn[:, :], kkn[:, :], rn[:, :], ALU.mult, [kkn, rn], [kkn])
                        kds = tF["kds"].get()
                        stage(5.21)
                        for d in range(2):
                            dp = slice(d * 64, (d + 1) * 64)
                            lw = tF["lw"].get()
                            inc = tF["inc"].get()
                            E = tF["E"].get()
                            ad = tF["ad"].get()
                            kd = tF["kd"].get()
                            bd = tF["bd"].get()
                            totc = totr.get()
                            ps = ps_next()
                            mm(ps[:, :], wsm[dp, 0, ccs], wlT[dp, cols], True, True, [wsm, wlT], [ps])
                            act(lw[:, :], ps[:, :], AF.Sigmoid, [ps, colt], [lw], bias=colt[:, 60 + d * 4 + cc:61 + d * 4 + cc])
                            ts("dve", lw[:, :], lw[:, :], -EM05, None, ALU.mult, None, [lw], [lw])
                            fw.op("dve", lambda e, inc=inc, lw=lw: e.tensor_tensor_scan(
                                out=inc[:, :], data0=cmask[:, 0:TG], data1=lw[:, :], initial=0.0, op0=ALU.mult, op1=ALU.add),
                                [cmask, lw], [inc])
                            cp("dve", totc[:, :], v3(inc[:, :])[:, :, 127], [inc], [totc])
                            tot_bc = bass.AP(totc.t, 0, [[4, 128], [1, 4], [0, 128]])
                            if d == 1:
                                tt("dve", inc[:, :], lw[:, :], inc[:, :], ALU.subtract, [lw, inc], [inc])
                                tt("dve", v3(inc[:, :]), v3(inc[:, :]), tot_bc, ALU.add, [inc, totc], [inc])
                            stage(5.22)
                            act(pC[d][:, c0:c0 + 4], totc[:, :], AF.Exp, [totc], [pC[d]])
                            act(E[:, :], inc[:, :], AF.Exp, [inc], [E])
                            tt("pool", AR[d][:, c0:c0 + 4, 1, :], v3(rT[:, cols]), v3(E[:, :]), ALU.mult, [rT, E], [AR[d]])
                            tt("pool", lw[:, :], inc[:, :], lw[:, :], ALU.subtract, [inc, lw], [lw])
                            act(E[:, :], lw[:, :], AF.Exp, [lw], [E])
                            stt(AR[d][:, c0:c0 + 4, 0, :], v3(kkn[:, :]), -1.0, v3(E[:, :]), ALU.mult, ALU.mult, [kkn, E], [AR[d]])
                            stage(5.23)
                            ps = ps_next()
                            mm(ps[:, :], wsm[dp, 1, ccs], alT[dp, cols], True, True, [wsm, alT], [ps])
                            act(ad[:, :], ps[:, :], AF.Sigmoid, [ps, colt], [ad], bias=colt[:, 68 + d * 4 + cc:69 + d * 4 + cc])
                            ts("dve", kd[:, :], ad[:, :], colt[:, 80 + cc:81 + cc], omka[:, cc:cc + 1], ALU.mult, ALU.add, [ad, colt, omka], [kd])
                            tt("pool", kd[:, :], kd[:, :], kT[:, cols], ALU.mult, [kd, kT], [kd])
                            tt("pool", bd[:, :], kkn[:, :], ad[:, :], ALU.mult, [kkn, ad], [bd])
                            if d == 0:
                                cp("pool", kds[:, :], kd[:, :], [kd], [kds])
                            else:
                                tt("pool", kds[:, :], kds[:, :], kd[:, :], ALU.add, [kds, kd], [kds])
                            stage(5.24)
                            act(E[:, :], inc[:, :], AF.Exp, [inc], [E], scale=-1.0)
                            tt("dve", KTl[d][:, cols], kd[:, :], E[:, :], ALU.mult, [kd, E], [KTl[d]])
                            tt("pool", BTl[d][:, cols], bd[:, :], E[:, :], ALU.mult, [bd, E], [BTl[d]])
                            tt("dve", v3(lw[:, :]), v3(inc[:, :]), tot_bc, ALU.subtract, [inc, totc], [lw])
                            act(E[:, :], lw[:, :], AF.Exp, [lw], [E], scale=-1.0)
                            stage(5.25)
                            KBf = KBfr.get()
                            tt("dve", KBf[:, 0, :], kd[:, :], E[:, :], ALU.mult, [kd, E], [KBf])
                            tt("pool", KBf[:, 1, :], bd[:, :], E[:, :], ALU.mult, [bd, E], [KBf])
                            for cl in range(4):
                                for w in range(2):
                                    tr(pbh[:, (cl * 2 + w) * 128:(cl * 2 + w + 1) * 128], KBf[:, w, cl * 128:(cl + 1) * 128], identb[:],
                                       [KBf, identb], [pbh])
                            cp("act", KBh[d][:, c0:c0 + 4, :, :], pbh[:, :].rearrange("p (c w t) -> p c w t", w=2, t=128), [pbh], [KBh[d]])
                        stage(5.26)
                        bt = tF["rn"].get()
                        btb = tH["btb"].get()
                        tt("pool", bt[:, :], rT[:, cols], kds[:, :], ALU.mult, [rT, kds], [bt])
                        stage(5.262)
                        ts("dve", btb[:, :], bt[:, :], colt[:, 84 + cc:85 + cc], None, ALU.mult, None, [bt, colt], [btb])
                        stage(5.263)
                        ps = ps_next()
                        mm(ps[:, :], blkb[:, :], btb[:, :], True, True, [blkb, btb], [ps])
                        stage(5.264)
                        cp("act", bt[:, :], ps[:, :], [ps], [bt])
                        tt("pool", bonusv[:, cols], bt[:, :], vT[:, cols], ALU.mult, [bt, vT], [bonusv])
                        stage(5.27)

                    stage(5.3)
                    fw.op("pool", lambda e: e.memset(Hf[:], 0.0), [Hf], [Hf])
                    fw.op("pool", lambda e: e.memset(Hb[:], 0.0), [Hb], [Hb])
                    ydir = [rT, kT]
                    A_h = [pb[0][:, :].rearrange("p (d f) -> p d f", f=256), pb[1][:, :].rearrange("p (d f) -> p d f", f=256)]
                    B_h = [pb[2][:, :].rearrange("p (d f) -> p d f", f=256), pb[3][:, :].rearrange("p (d f) -> p d f", f=256)]
                    C_h = [pb[4][:, 0:256].rearrange("p (d f) -> p d f", f=128), pb[5][:, 0:256].rearrange("p (d f) -> p d f", f=128)]

                    def hv(ap):
                        return ap.rearrange("p (d h) f -> p h d f", h=2)
                    S5_v = pb[5][:, :].rearrange("p (q f) -> p q f", f=128)
                    S6_v = pb[6][:, :].rearrange("p (q f) -> p q f", f=128)
                    Z0_v = pb[0][:, 0:256].rearrange("p (q f) -> p q f", f=64)
                    ZC_v = pb[1][:, :].rearrange("p (q f) -> p q f", f=128)
                    U_h = [pb[2][:, 0:128].rearrange("p (d f) -> p d f", f=64), pb[3][:, 0:128].rearrange("p (d f) -> p d f", f=64)]
                    Y_h = [pb[0][:, 256:512].rearrange("p (d f) -> p d f", f=128), pb[1][:, 256:512].rearrange("p (d f) -> p d f", f=128)]
                    H_v = pb[4][:, 256:384].rearrange("p (d f) -> p d f", f=64)
                    for n in range(16):
                        cs = [n, 15 - n]
                        QS = [(q, q // 2, q % 2, slice((q % 2) * 64, (q % 2) * 64 + 64), cs[q // 2]) for q in range(4)]
                        for (q, d, hh, P, c) in QS:
                            if hh == 1 and _os.environ.get("HH0"):
                                continue
                            ck = slice(c * 128, (c + 1) * 128)
                            arf = AR[d][P, c, :, :].rearrange("p a t -> p (a t)")
                            mm(A_h[hh][:, d, :], BTl[d][P, ck], arf, True, True, [BTl[d], AR[d]], [pb[hh]])
                            mm(B_h[hh][:, d, :], KTl[d][P, ck], arf, True, True, [KTl[d], AR[d]], [pb[2 + hh]])
                            mm(C_h[hh][:, d, :], AR[d][P, c, 0, :], BTl[d][P, ck], True, True, [AR[d], BTl[d]], [pb[4 + hh]])
                        for hh in range(2):
                            tt("dve", hv(LM_A[:, :, :])[:, hh], A_h[hh], hv(mAB)[:, hh], ALU.mult, [pb[hh], cst], [LM_A])
                            tt("dve", hv(LM_B[:, :, :])[:, hh], B_h[hh], hv(mAB)[:, hh], ALU.mult, [pb[2 + hh], cst], [LM_B])
                            tt("dve", hv(Pq[0][:, :, :])[:, hh], C_h[hh], hv(mC)[:, hh], ALU.mult, [pb[4 + hh], cst], [Pq[0]])
                        tt("pool", IPT[:, 0, :, :], LM_A[:, :, 0:128], identq[:, :, :], ALU.add, [LM_A, identq], [IPTs[0]])
                        Pk = Pq[0]
                        PTk_ap = LM_A[:, :, 0:128]
                        PTk_buf = LM_A
                        for k in range(6):
                            Pn = Pq[(k + 1) % 2]
                            PTn = PTq[(k + 1) % 2]
                            for q in range(4):
                                mm(S5_v[:, q, :], Pk[:, q, :], PTk_ap[:, q, :], True, True, [Pk, PTk_buf], [pb[5]])
                            if k < 5:
                                for q in range(4):
                                    mm(S6_v[:, q, :], PTk_ap[:, q, :], Pk[:, q, :], True, True, [Pk, PTk_buf], [pb[6]])
                            cp("act", PTn[:, :, :], S5_v, [pb[5]], [PTn])
                            if k < 5:
                                cp("dve", Pn[:, :, :], S6_v, [pb[6]], [Pn])
                            tt("pool", IPT[:, k + 1, :, :], PTn[:, :, :], identq[:, :, :], ALU.add, [PTn, identq], [IPTs[k + 1]])
                            Pk = Pn
                            PTk_ap = PTn[:, :, :]
                            PTk_buf = PTn
                        for d in range(2):
                            tr(pbh[:, d * 128:(d + 1) * 128], AR[d][:, cs[d], 0, :], identb[:], [AR[d], identb], [pbh])
                        for (q, d, hh, P, c) in QS:
                            mm(Z0_v[:, q, :], LM_B[:, q, 0:128], Vtm[:, c, hh * 64:(hh + 1) * 64], True, True, [LM_B, Vtm], [pb[0]])
                        cp("act", Zq[0][:, :, 0:64], pbh[:, 0:256].rearrange("p (q j) -> p q j", j=64), [pbh], [Zq[0]])
                        cp("dve", Zq[0][:, :, 64:128], Z0_v, [pb[0]], [Zq[0]])
                        for k in range(7):
                            Zc = Zq[k % 2]
                            Zn = Zq[(k + 1) % 2]
                            for q in range(4):
                                mm(ZC_v[:, q, :], IPT[:, k, q, :], Zc[:, q, :], True, True, [IPTs[k], Zc], [pb[1]])
                            if k < 6:
                                _v = _os.environ.get("EXPV", "")
                                if _v == "c":
                                    cp("act", LM_B[:, :, 0:128], ZC_v, [pb[1]], [LM_B])
                                elif _v == "d":
                                    cp("act", Zn[:, :, :], S5_v, [pb[5]], [Zn])
                                else:
                                    cp("act", Zn[:, :, :], ZC_v, [pb[1]], [Zn])
                            else:
                                cp("act", Ap[:, :, :], ZC_v[:, :, 0:64], [pb[1]], [Ap])
                                cp("dve", UV[:, :, :], ZC_v[:, :, 64:128], [pb[1]], [UV])
                        for d in range(2):
                            tr(pbh[:, 256 + d * 128:256 + (d + 1) * 128], Ap[:, 2 * d:2 * d + 2, :].rearrange("p a j -> p (a j)"), identb[:],
                               [Ap, identb], [pbh])
                        cp("act", ApT[:, :, :], pbh[:, 256:512].rearrange("p (d t) -> p d t", t=128), [pbh], [ApT])
                        for (q, d, hh, P, c) in QS:
                            mm(U_h[hh][:, d, :], ApT[P, d, :], Hb[P, d, :], True, True, [ApT, Hb], [pb[2 + hh]])
                        for hh in range(2):
                            tt("dve", hv(Uq[:, :, :])[:, hh], U_h[hh], hv(UV[:, :, :])[:, hh], ALU.add, [pb[2 + hh], UV], [Uq])
                        for (q, d, hh, P, c) in QS:
                            hs = slice(hh * 64, (hh + 1) * 64)
                            mm(Y_h[hh][P, d, :], Hb[P, d, :], AR[d][P, c, 1, :], True, False, [Hb, AR[d]], [pb[hh]])
                            mm(Y_h[hh][P, d, :], Uq[:, q, :], LM_A[:, q, 128:256], False, False, [Uq, LM_A], [pb[hh]])
                            mm(Y_h[hh][P, d, :], Vtm[:, c, hs], LM_B[:, q, 128:256], False, True, [Vtm, LM_B], [pb[hh]])
                        for (q, d, hh, P, c) in QS:
                            hs = slice(hh * 64, (hh + 1) * 64)
                            mm(H_v[P, d, :], KBh[d][:, c, 1, hs], Uq[:, q, :], True, False, [KBh[d], Uq], [pb[4]])
                            mm(H_v[P, d, :], KBh[d][:, c, 0, hs], Vtm[:, c, hs], False, True, [KBh[d], Vtm], [pb[4]])
                        for d in range(2):
                            c = cs[d]
                            for hh in range(2):
                                Ph = slice(hh * 64, hh * 64 + 64)
                                cp("act", ydir[d][Ph, c * 128:(c + 1) * 128], Y_h[hh][Ph, d, :], [pb[hh]], [ydir[d]])
                            stt(Hf[:, d, :], Hf[:, d, :], pC[d][:, c:c + 1], H_v[:, d, :], ALU.mult, ALU.add, [Hf, pC[d], pb[4]], [Hf])
                        cp("act", Hb[:, :, :], Hf[:, :, :], [Hf], [Hb])
                        stage(5.4)

                    if _os.environ.get("DBG_BV"):
                      if _os.environ.get("DBG_BV") == "2":
                        fw.dma("sp", xres[0:128, 0:S], bonusv[:, :], reads=[bonusv], writes=[xres_sub[0][0]])
                        fw.dma("sp", xres[128:256, 0:S], rT[:, :], reads=[rT], writes=[xres_sub[0][0]])
                        fw.dma("sp", xres[256:384, 0:S], kT[:, :], reads=[kT], writes=[xres_sub[0][0]])
                    stage(5.5)
                    blkf = cst[:, 128:256]
                    for tg in range(NTG):
                        cols = tgs(tg)
                        y0 = ydir[0]
                        tt("pool", y0[:, cols], y0[:, cols], ydir[1][:, cols], ALU.add, [y0, ydir[1]], [y0])
                        t = eT["t"].get()
                        mean = eT["mean"].get()
                        msq = eT["msq"].get()
                        var = eT["var"].get()
                        yh = tH["sqb"].get()
                        yl = tH["btb"].get()
                        cp("act", yh[:, :], y0[:, cols], [y0], [yh])
                        tt("pool", t[:, :], y0[:, cols], yh[:, :], ALU.subtract, [y0, yh], [t])
                        cp("act", yl[:, :], t[:, :], [t], [yl])
                        psm = ps_next()
                        mm(psm[:, :], blkb[:, :], yh[:, :], True, False, [blkb, yh], [psm])
                        mm(psm[:, :], blkb[:, :], yl[:, :], False, True, [blkb, yl], [psm])
                        ts("dve", mean[:, :], psm[:, :], 1.0 / 64, None, ALU.mult, None, [psm], [mean])
                        tt("pool", t[:, :], y0[:, cols], mean[:, :], ALU.subtract, [y0, mean], [t])
                        act(yh[:, :], t[:, :], AF.Square, [t], [yh])
                        psq = ps_next()
                        mm(psq[:, :], blkb[:, :], yh[:, :], True, True, [blkb, yh], [psq])
                        act(var[:, :], psq[:, :], AF.Sqrt, [psq], [var], bias=EPSG, scale=1.0 / 64)
                        recip(var[:, :], var[:, :], [var], [var])
                        tt("pool", t[:, :], t[:, :], var[:, :], ALU.mult, [t, var], [t])
                        ts("dve", t[:, :], t[:, :], colt[:, 88 + cc:89 + cc], colt[:, 92 + cc:93 + cc], ALU.mult, ALU.add, [t, colt], [t])
                        tt("pool", t[:, :], t[:, :], bonusv[:, cols], ALU.add, [t, bonusv], [t])
                        psg = ps_next()
                        mm(psg[:, :], wsm[:, 2, ccs], glT[:, cols], True, True, [wsm, glT], [psg])
                        yc = ycr.get()
                        tt("dve", yc[:, :], t[:, :], psg[:, :], ALU.mult, [t, psg], [yc])
                        fw.dma("sp", ymix[2, ccs, cols], yc[:, :], reads=[yc], writes=[ymix_sub[2]])

        xres_sub = [[Buf(xres.t, "xres_%d_%d" % (s_, g_)) for g_ in range(NTG)] for s_ in range(NSEQ)]
        ymix_sub = [Buf(ymix.t, "ymix%d" % i) for i in range(3)]
        WB = Buf(None, "weights")

        def gelu(ps_ap, out_ap, shape, tmpA, tmpB, reads, writes):
            xs = tmpA.get()
            t1 = tmpB.get()
            xs_ap = xs[:, 0:shape]
            t1_ap = t1[:, 0:shape]
            cp("act", xs_ap, ps_ap, reads, [xs])
            act(t1_ap, ps_ap, AF.Square, reads, [t1])
            ts("pool", t1_ap, t1_ap, 0.044715, 1.0, ALU.mult, ALU.add, [t1], [t1])
            tt("pool", t1_ap, t1_ap, xs_ap, ALU.mult, [t1, xs], [t1])
            act(t1_ap, t1_ap, AF.Sigmoid, [t1], [t1], scale=C_SIG)
            tt("dve", out_ap, t1_ap, xs_ap, ALU.mult, [t1, xs], writes)

        def layer_body(s, l):
            W = Wt[l]
            t_off = s * S
            last = (l == NL - 1)
            with fw.scope():
                colt = fw.sbuf("colt", [128, NCOLS], F32)
                fw.dma("sp", colt[:], cols_in[l, :, :], writes=[colt])
                hT = fw.sbuf("hT", [128, 8, S], BF16)
                hTs = fw.sub(hT, NTG)

                def tgs(tg):
                    return slice(tg * TG, (tg + 1) * TG)

                stage(2)
                with fw.scope():
                    xr = Ring("xa", [128, 8, TG], F32, 2)
                    sq = fw.sbuf("sqa", [128, 8, TG], BF16)
                    rs = fw.sbuf("rsa", [128, TG], F32)
                    for tg in range(NTG):
                        xt = xr.get()
                        fw.dma("sp", xt[:], xres_view(t_off + tg * TG, TG), reads=[xres_sub[s][tg]], writes=[xt])
                        rms_feature(xt[:, :, :], xt, 8, colt[:, 0:8], 1.0 / D, EPS6,
                                    lambda c, tg=tg: (hT[:, c, tgs(tg)], [hTs[tg]]), sq, rs, TG, gb=colt)

                stage(3)
                import os as _os
                SKIP = bool(_os.environ.get("RW_ONLY"))
                with fw.scope():
                  if not SKIP:
                      wm = fw.sbuf("wm", [128, 5, 8, 128], BF16)
                      fw.dma("sp", wm[:].rearrange("p t c j -> p t (c j)"),
                             W["mla"][:, :, :, :].rearrange("t p c j -> p t (c j)"), reads=[WB], writes=[wm])
                      wkr = fw.sbuf("wkr", [128, 8, 64], BF16)
                      fw.dma("sp", wkr[:], W["kr"][:, :, :], reads=[WB], writes=[wkr])
                      wuq = fw.sbuf("wuq", [128, 3, 768], BF16)
                      fw.dma("sp", wuq[:], W["uq"][:, :, :], reads=[WB], writes=[wuq])
                      wuqsw = fw.sbuf("wuqsw", [128, 3, 256], BF16)
                      fw.dma("sp", wuqsw[:], W["uqsw"][:, :, :], reads=[WB], writes=[wuqsw])
                      wukv = fw.sbuf("wukv", [128, 2, 1024], BF16)
                      fw.dma("sp", wukv[:], W["ukv"][:, :, :], reads=[WB], writes=[wukv])
                      cqn = fw.sbuf("cqn", [128, 3, S], BF16)
                      ckvn = fw.sbuf("ckvn", [128, 2, S], BF16)
                      krot = fw.sbuf("krot", [96, S], BF16)
                      CC = fw.sbuf("CC", [96, S], BF16)
                      SS = fw.sbuf("SS", [96, S], BF16)
                      V1 = fw.sbuf("V1", [128, 16, 8, 65], BF16)
                      Oall = fw.sbuf("Oall", [128, 16, 512], BF16)
                      R = slice(64, 96)
                      with fw.scope():
                          posi = fw.sbuf("posi", [96, S], I32)
                          ru = fw.sbuf("ru", [96, S], F32)
                          ruf = fw.sbuf("ruf", [96, S], F32)
                          fw.dma("sp", posi[R, :], pos_in[s:s + 1, :].partition_broadcast(32), writes=[posi])
                          cp("dve", ru[R, :], posi[R, :], [posi], [ru])
                          ts("dve", ru[R, :], ru[R, :], cst[R, 1792:1793], 1.0 / (2 * math.pi), ALU.mult, ALU.mult, [ru, cst], [ru])
                          TWO_PI = 2 * math.pi * (1 - 1e-6)
                          for which in range(2):
                              if which == 1:
                                  ts("dve", ru[R, :], ru[R, :], 0.25, None, ALU.add, None, [ru], [ru])
                              cp("dve", posi[R, :], ru[R, :], [ru, posi], [posi])
                              cp("dve", ruf[R, :], posi[R, :], [posi], [ruf])
                              tt("dve", ruf[R, :], ru[R, :], ruf[R, :], ALU.subtract, [ru, ruf], [ruf])
                              dst = SS if which == 0 else CC
                              act(dst[R, :], ruf[R, :], AF.Sin, [ruf], [dst], scale=TWO_PI)
                          ts("dve", SS[R, :], SS[R, :], cst[R, 1793:1794], None, ALU.mult, None, [SS, cst], [SS])
                      fw.op("pool", lambda e: e.memset(V1[:, :, :, 64:65], 1.0), [], [V1])
                      with fw.scope():
                          cqr = Ring("cqf", [128, 5, TG], F32, 1)
                          sq = fw.sbuf("sqm", [128, 5, TG], BF16)
                          rs = fw.sbuf("rsm", [128, TG], F32)
                          t1r = Ring("t1m", [96, TG], F32, 2)
                          t2r = Ring("t2m", [96, TG], F32, 2)
                          for tg in range(NTG):
                              cqf = cqr.get()
                              for n in range(5):
                                  ps = ps_next()
                                  for c in range(8):
                                      mm(ps[:, :], wm[:, n, c, :], hT[:, c, tgs(tg)], c == 0, c == 7, [wm, hTs[tg]], [ps])
                                  cp("act", cqf[:, n, :], ps[:, :], [ps], [cqf])
                              rms_feature(cqf[:, 0:3, :], cqf, 3, colt[:, 40:43], 1.0 / 384, EPS6,
                                          lambda c, tg=tg: (cqn[:, c, tgs(tg)], [cqn]), sq, rs, TG, gb=colt)
                              rms_feature(cqf[:, 3:5, :], cqf, 2, colt[:, 43:45], 1.0 / 256, EPS6,
                                          lambda c, tg=tg: (ckvn[:, c, tgs(tg)], [ckvn]), sq, rs, TG, gb=colt)
                              psk = ps_next()
                              psk2 = ps_next()
                              for c in range(8):
                                  mm(psk[R, :], wkr[:, c, 0:32], hT[:, c, tgs(tg)], c == 0, c == 7, [wkr, hTs[tg]], [psk])
                              for c in range(8):
                                  mm(psk2[R, :], wkr[:, c, 32:64], hT[:, c, tgs(tg)], c == 0, c == 7, [wkr, hTs[tg]], [psk2])
                              t1 = t1r.get()
                              t2 = t2r.get()
                              tt("dve", t1[R, :], psk[R, :], CC[R, tgs(tg)], ALU.mult, [psk, CC], [t1])
                              tt("dve", t2[R, :], psk2[R, :], SS[R, tgs(tg)], ALU.mult, [psk2, SS], [t2])
                              tt("pool", krot[R, tgs(tg)], t1[R, :], t2[R, :], ALU.add, [t1, t2], [krot])
                              for kt in range(tg * 4, tg * 4 + 4):
                                  ps = ps_next()
                                  psv = ps[:, :].rearrange("p (h e) -> p h e", e=64)
                                  for c in range(2):
                                      mm(psv, ckvn[:, c, kt * 128:(kt + 1) * 128],
                                         wukv[:, c, :].rearrange("p (h e) -> p h e", e=128)[:, :, 64:128],
                                         c == 0, c == 1, [ckvn, wukv], [ps])
                                  cp("act", V1[:, kt, :, 0:64], psv, [ps], [V1])
                          QTr = Ring("QT", [96, S], BF16, 2)
                          KTr = Ring("KT", [96, S], BF16, 2)
                          PTr = Ring("PT", [128, 16, TG], BF16, 2)
                          rcr = Ring("rc", [128, 4], F32, 2)
                          pso = pb[6]
                          pso_v = pso[:, 0:260].rearrange("p (q e) -> p q e", e=65)
                          for h in range(8):
                              QT = QTr.get()
                              KT = KTr.get()
                              for tg in range(NTG):
                                  psq = ps_next()
                                  psq2 = ps_next()
                                  for c in range(3):
                                      mm(psq[0:96, :], wuq[:, c, h * 96:(h + 1) * 96], cqn[:, c, tgs(tg)], c == 0, c == 2, [wuq, cqn], [psq])
                                  for c in range(3):
                                      mm(psq2[R, :], wuqsw[:, c, h * 32:(h + 1) * 32], cqn[:, c, tgs(tg)], c == 0, c == 2, [wuqsw, cqn], [psq2])
                                  cp("act", QT[0:64, tgs(tg)], psq[0:64, :], [psq], [QT])
                                  t1 = t1r.get()
                                  t2 = t2r.get()
                                  tt("dve", t1[R, :], psq[R, :], CC[R, tgs(tg)], ALU.mult, [psq, CC], [t1])
                                  tt("dve", t2[R, :], psq2[R, :], SS[R, tgs(tg)], ALU.mult, [psq2, SS], [t2])
                                  tt("pool", QT[R, tgs(tg)], t1[R, :], t2[R, :], ALU.add, [t1, t2], [QT])
                                  psk = ps_next()
                                  for c in range(2):
                                      mm(psk[0:64, :], wukv[:, c, h * 128:h * 128 + 64], ckvn[:, c, tgs(tg)], c == 0, c == 1, [wukv, ckvn], [psk])
                                  cp("act", KT[0:64, tgs(tg)], psk[0:64, :], [psk], [KT])
                              cp("pool", KT[R, :], krot[R, :], [krot], [KT])
                              for qg in range(NTG):
                                  PT = PTr.get()
                                  for kt in range(16):
                                      ps = ps_next()
                                      mm(ps[:, :], KT[0:96, kt * 128:(kt + 1) * 128], QT[0:96, tgs(qg)], True, True, [KT, QT], [ps])
                                      act(PT[:, kt, :], ps[:, :], AF.Exp, [ps], [PT], scale=SCALE)
                                  for qt in range(4):
                                      for kt in range(16):
                                          mm(pso_v[:, qt, :], PT[:, kt, qt * 128:(qt + 1) * 128], V1[:, kt, h, :], kt == 0, kt == 15, [PT, V1], [pso])
                                  rc = rcr.get()
                                  recip(rc[:, :], pso_v[:, :, 64], [pso], [rc])
                                  for qt in range(4):
                                      ts("dve", Oall[:, qg * 4 + qt, h * 64:(h + 1) * 64], pso_v[:, qt, 0:64], rc[:, qt:qt + 1], None,
                                         ALU.mult, None, [pso, rc], [Oall])
                          ya = Ring("yat", [128, 1024], BF16, 2)
                          for cc in range(4):
                              for half in range(2):
                                  for j in range(8):
                                      tile = half * 8 + j
                                      tr(pbh[:, j * 128:(j + 1) * 128], Oall[:, tile, cc * 128:(cc + 1) * 128], identb[:], [Oall, identb], [pbh])
                                  yt = ya.get()
                                  cp("act" if half else "dve", yt[:], pbh[:, :], [pbh], [yt])
                                  fw.dma("sp", ymix[0, cc * 128:(cc + 1) * 128, half * 1024:(half + 1) * 1024], yt[:], reads=[yt], writes=[ymix_sub[0]])

                stage(4)
                with fw.scope():
                  if not SKIP:
                      wsu = fw.sbuf("wsu", [128, 4, 8, 128], BF16)
                      fw.dma("sp", wsu[:].rearrange("p t c j -> p t (c j)"),
                             W["sgu"][:, :, :, :].rearrange("t p c j -> p t (c j)"), reads=[WB], writes=[wsu])
                      wsv = fw.sbuf("wsv", [128, 8, 512], BF16)
                      fw.dma("sp", wsv[:], W["sgv"][:, :, :], reads=[WB], writes=[wsv])
                      wsw = fw.sbuf("wsw", [128, 8, 128], BF16)
                      fw.dma("sp", wsw[:].rearrange("p g t -> p (g t)"), W["sgw"][:, :], reads=[WB], writes=[wsw])
                      sgbt = fw.sbuf("sgbt", [128, 4, 128], F32)
                      fw.dma("sp", sgbt[:], sgb_in[l, :, :, :], writes=[sgbt])
                      lng = fw.sbuf("lng", [128, 512], F32)
                      lnb = fw.sbuf("lnb", [128, 512], F32)
                      fw.dma("sp", lng[:], sgln_in[l, 0:1, :].partition_broadcast(128), writes=[lng])
                      fw.dma("sp", lnb[:], sgln_in[l, 1:2, :].partition_broadcast(128), writes=[lnb])
                      uT = fw.sbuf("uT", [128, 4, S], BF16)
                      tA = Ring("gA", [128, 512], F32, 2)
                      tB = Ring("gB", [128, 512], F32, 2)
                      for tg in range(NTG):
                          for n in range(4):
                              ps = ps_next()
                              for c in range(8):
                                  mm(ps[:, :], wsu[:, n, c, :], hT[:, c, tgs(tg)], c == 0, c == 7, [wsu, hTs[tg]], [ps])
                              gelu(ps[:, :], uT[:, n, tgs(tg)], 512, tA, tB, [ps], [uT])
                      vgr = Ring("vg", [128, 512], F32, 2)
                      vlr = Ring("vln", [128, 512], BF16, 2)
                      st6r = Ring("st6", [128, 6], F32, 2)
                      mvr = Ring("mv", [128, 2], F32, 2)
                      r1r = Ring("r1", [128, 1], F32, 2)
                      tmr = Ring("tmy", [128, 4, 128], F32, 2)
                      ybr = Ring("ybt", [128, 4, 128], BF16, 2)
                      for kt in range(16):
                          ps = ps_next()
                          for c in range(8):
                              mm(ps[:, :], hT[:, c, kt * 128:(kt + 1) * 128], wsv[:, c, :], c == 0, c == 7, [hTs[kt // 4], wsv], [ps])
                          vg = vgr.get()
                          gelu(ps[:, :], vg[:, :], 512, tA, tB, [ps], [vg])
                          st6 = st6r.get()
                          mv = mvr.get()
                          r1 = r1r.get()
                          fw.op("dve", lambda e, st6=st6, vg=vg: e.bn_stats(out=st6[:, :], in_=vg[:, :]), [vg], [st6])
                          fw.op("dve", lambda e, st6=st6, mv=mv: e.bn_aggr(out=mv[:, :], in_=st6[:, :]), [st6], [mv])
                          act(r1[:, :], mv[:, 1:2], AF.Sqrt, [mv], [r1], bias=EPS5, scale=1.0)
                          recip(r1[:, :], r1[:, :], [r1], [r1])
                          ts("dve", vg[:, :], vg[:, :], mv[:, 0:1], r1[:, 0:1], ALU.subtract, ALU.mult, [vg, mv, r1], [vg])
                          tt("pool", vg[:, :], vg[:, :], lng[:, :], ALU.mult, [vg, lng], [vg])
                          vln = vlr.get()
                          tt("pool", vln[:, :], vg[:, :], lnb[:, :], ALU.add, [vg, lnb], [vln])
                          psm = ps_next()
                          psm_v = psm[:, :].rearrange("p (g t) -> p g t", t=128)
                          for g in range(8):
                              hh = g % 2
                              mm(psm_v[hh * 64:(hh + 1) * 64, g // 2, :], vln[:, g * 64:(g + 1) * 64], wsw[:, g, :], True, True, [vln, wsw], [psm])
                          tmy = tmr.get()
                          tt("dve", tmy[:, :, :], psm_v, sgbt[:, :, :], ALU.add, [psm, sgbt], [tmy])
                          ybt = ybr.get()
                          tt("pool", ybt[:, :, :], tmy[:, :, :], uT[:, :, kt * 128:(kt + 1) * 128], ALU.mult, [tmy, uT], [ybt])
                          fw.dma("sp", ymix[1, :, :].rearrange("(n p) t -> p n t", p=128)[:, :, kt * 128:(kt + 1) * 128], ybt[:, :, :],
                                 reads=[ybt], writes=[ymix_sub[1]])

                stage(5)
                phase_rwkv(s, l, W, colt, hT, hTs, tgs)

                stage(6)
                with fw.scope():
                    wg_r = Ring("wg", [128, 8, 128], BF16, 4)
                    wb_r = Ring("wb", [128, 4, 128], BF16, 4)
                    wo_r = Ring("wo", [128, 8, 128], BF16, 3)
                    wf_r = Ring("wf", [128, 8, 128], BF16, 6)
                    wd_r = Ring("wd", [128, NF, 128], BF16, 2)
                    ymr = Ring("ym", [128, 3, 4, TG], BF16, 1)
                    xr = Ring("xm", [128, 8, TG], F32, 1)
                    sq = fw.sbuf("sqf", [128, 8, TG], BF16)
                    rs = fw.sbuf("rsf", [128, TG], F32)
                    gtr = Ring("gt", [128, TG], F32, 3)
                    accr = Ring("acc", [128, TG], F32, 2)
                    tmpr = Ring("tmpm", [128, TG], F32, 2)
                    mT = fw.sbuf("mT", [128, 8, TG], BF16)
                    h2 = fw.sbuf("h2T", [128, 8, TG], BF16)
                    aT = fw.sbuf("aT", [128, NF, TG], BF16)
                    sgr = Ring("sgf", [128, TG], F32, 2)
                    if last:
                        yfin = fw.sbuf("yfin", [128, 8, TG], F32)
                        otr = Ring("otr", [128, D], F32, 2)
                    for tg in range(NTG):
                        ym = ymr.get()
                        for n in range(3):
                            fw.dma("sp", ym[:, n, :, :], ymix[n, :, :].rearrange("(c p) t -> p c t", p=128)[:, :, tgs(tg)],
                                   reads=[ymix_sub[n]], writes=[ym])
                        xt = xr.get()
                        fw.dma("sp", xt[:], xres_view(t_off + tg * TG, TG), reads=[xres_sub[s][tg]], writes=[xt])
                        for dc in range(8):
                            acc = accr.get()
                            for n in range(3):
                                wg = wg_r.get()
                                fw.dma("sp", wg[:], W["gt"][n * 8 + dc, :, :, :], reads=[WB], writes=[wg])
                                wb_ = wb_r.get()
                                fw.dma("sp", wb_[:], W["br"][n * 8 + dc, :, :, :], reads=[WB], writes=[wb_])
                                psg = ps_next()
                                for c in range(8):
                                    mm(psg[:, :], wg[:, c, :], hT[:, c, tgs(tg)], c == 0, c == 7, [wg, hTs[tg]], [psg])
                                psb = ps_next()
                                for c in range(4):
                                    mm(psb[:, :], wb_[:, c, :], ym[:, n, c, :], c == 0, c == 3, [wb_, ym], [psb])
                                gt = gtr.get()
                                act(gt[:, :], psg[:, :], AF.Sigmoid, [psg, colt], [gt], bias=colt[:, 16 + n * 8 + dc:17 + n * 8 + dc], scale=1.0)
                                if n == 0:
                                    tt("dve", acc[:, :], gt[:, :], psb[:, :], ALU.mult, [gt, psb], [acc])
                                else:
                                    tm = tmpr.get()
                                    tt("dve", tm[:, :], gt[:, :], psb[:, :], ALU.mult, [gt, psb], [tm])
                                    if n == 1:
                                        tt("pool", acc[:, :], acc[:, :], tm[:, :], ALU.add, [acc, tm], [acc])
                                    else:
                                        tt("pool", mT[:, dc, :], acc[:, :], tm[:, :], ALU.add, [acc, tm], [mT])
                        for dc in range(8):
                            wo = wo_r.get()
                            fw.dma("sp", wo[:], W["wo"][dc, :, :, :], reads=[WB], writes=[wo])
                            ps = ps_next()
                            for c in range(8):
                                mm(ps[:, :], wo[:, c, :], mT[:, c, :], c == 0, c == 7, [wo, mT], [ps])
                            tt("dve", xt[:, dc, :], xt[:, dc, :], ps[:, :], ALU.add, [xt, ps], [xt])
                        rms_feature(xt[:, :, :], xt, 8, colt[:, 8:16], 1.0 / D, EPS6,
                                    lambda c: (h2[:, c, :], [h2]), sq, rs, TG, gb=colt)
                        for f in range(NF):
                            wfg = wf_r.get()
                            fw.dma("sp", wfg[:], W["fg"][f, :, :, :], reads=[WB], writes=[wfg])
                            wfu = wf_r.get()
                            fw.dma("sp", wfu[:], W["fu"][f, :, :, :], reads=[WB], writes=[wfu])
                            psg = ps_next()
                            for c in range(8):
                                mm(psg[:, :], wfg[:, c, :], h2[:, c, :], c == 0, c == 7, [wfg, h2], [psg])
                            psu = ps_next()
                            for c in range(8):
                                mm(psu[:, :], wfu[:, c, :], h2[:, c, :], c == 0, c == 7, [wfu, h2], [psu])
                            sg = sgr.get()
                            act(sg[:, :], psg[:, :], AF.Silu, [psg], [sg])
                            tt("dve", aT[:, f, :], sg[:, :], psu[:, :], ALU.mult, [sg, psu], [aT])
                        for dc in range(8):
                            wd = wd_r.get()
                            fw.dma("sp", wd[:], W["fd"][dc, :, :, :], reads=[WB], writes=[wd])
                            ps = ps_next()
                            for f in range(NF):
                                mm(ps[:, :], wd[:, f, :], aT[:, f, :], f == 0, f == NF - 1, [wd, aT], [ps])
                            tt("dve", xt[:, dc, :], xt[:, dc, :], ps[:, :], ALU.add, [xt, ps], [xt])
                        if not last or dbg:
                            fw.dma("sp", xres_view(t_off + tg * TG, TG), xt[:], reads=[xt], writes=[xres_sub[s][tg]])
                        if last:
                            rms_feature(xt[:, :, :], xt, 8, fing[:, 0:8], 1.0 / D, EPS6,
                                        lambda c: (yfin[:, c, :], [yfin]), sq, rs, TG, gb=fing)
                            for t4 in range(4):
                                ot = otr.get()
                                for half in range(2):
                                    ps = ps_next()
                                    for c4 in range(4):
                                        c = half * 4 + c4
                                        tr(ps[:, c4 * 128:(c4 + 1) * 128], yfin[:, c, t4 * 128:(t4 + 1) * 128], identf, [yfin, cst], [ps])
                                    cp("act" if half else "dve", ot[:, half * 512:(half + 1) * 512], ps[:, :], [ps], [ot])
                                r0 = t_off + tg * TG + t4 * 128
                                fw.dma("sp", out_d[r0:r0 + 128, :], ot[:], reads=[ot], writes=[Buf(None)])

        try:
            for s in range(NSEQ):
                for l in range(NL):
                    layer_body(s, l)
        except _Stop:
            pass
        fw.finish()
    return nc


_NC_CACHE = {}


def _colpack(v):
    v = np.asarray(v, np.float32).reshape(-1, 128)
    return np.ascontiguousarray(v.T)


def _host_layout(inp, NL=2):
    f = lambda a: np.ascontiguousarray(np.asarray(a, dtype=np.float32))
    cols = np.zeros((NL, 128, NCOLS), np.float32)
    for l in range(NL):
        c = cols[l]
        c[:, 0:8] = _colpack(inp["attn_norm_g"][l])
        c[:, 8:16] = _colpack(inp["ffn_norm_g"][l])
        c[:, 16:40] = _colpack(inp["gate_b"][l])
        c[:, 40:43] = _colpack(inp["q_norm_g"][l])
        c[:, 43:45] = _colpack(inp["kv_norm_g"][l])
        c[:, 45:60] = _colpack(inp["rw_mu"][l])
        c[:, 60:68] = _colpack(inp["rw_w0"][l])
        c[:, 68:76] = _colpack(inp["rw_a0"][l])
        c[:, 76:80] = _colpack(inp["rw_k_k"][l])
        c[:, 80:84] = _colpack(inp["rw_k_a"][l])
        c[:, 84:88] = _colpack(inp["rw_r_k"][l])
        c[:, 88:92] = _colpack(inp["rw_ln_g"][l])
        c[:, 92:96] = _colpack(inp["rw_ln_b"][l])
    fing = _colpack(inp["final_norm_g"])
    cst = np.zeros((128, CW), np.float32)
    p = np.arange(128)[:, None]
    fcol = np.arange(128)[None, :]
    cst[:, 0:128] = np.eye(128, dtype=np.float32)
    cst[:, 128:256] = ((p // 64) == (fcol // 64)).astype(np.float32)
    mab = np.zeros((128, 4, 256), np.float32)
    mc = np.zeros((128, 4, 128), np.float32)
    for q in range(4):
        d = q // 2
        if d == 0:
            mab[:, q, 0:128] = (p < fcol)
            mab[:, q, 128:256] = (p <= fcol)
            mc[:, q, :] = (fcol < p)
        else:
            mab[:, q, 0:128] = (p > fcol)
            mab[:, q, 128:256] = (p >= fcol)
            mc[:, q, :] = (fcol > p)
    cst[:, 256:1280] = mab.reshape(128, 1024)
    cst[:, 1280:1792] = mc.reshape(128, 512)
    inv_freq = (1.0 / (np.float32(10000.0) ** (np.arange(0, 32, 2, dtype=np.float32) / np.float32(32)))).astype(np.float32)
    for pp in range(64, 96):
        cst[pp, 1792] = inv_freq[(pp - 64) % 16]
        cst[pp, 1793] = -1.0 if pp < 80 else 1.0
    sgb = np.zeros((NL, 128, 4, 128), np.float32)
    sgwT = np.zeros((NL, 128, 1024), np.float32)
    sgln = np.zeros((NL, 2, 512), np.float32)
    w_uqsw = np.zeros((NL, 384, 256), np.float32)
    w_krsw = np.zeros((NL, 1024, 32), np.float32)
    perm = np.concatenate([np.arange(16, 32), np.arange(0, 16)])
    for l in range(NL):
        sb = np.asarray(inp["sg_b"][l], np.float32)
        for pp in range(128):
            for gi in range(4):
                sgb[l, pp, gi, :] = sb[2 * gi + pp // 64, :]
        sgwT[l] = np.asarray(inp["sg_w"][l], np.float32).transpose(2, 0, 1).reshape(128, 1024)
        sgln[l, 0] = inp["sg_ln_g"][l]
        sgln[l, 1] = inp["sg_ln_b"][l]
        wq = np.asarray(inp["w_uq"][l], np.float32)
        for h in range(8):
            w_uqsw[l, :, h * 32:(h + 1) * 32] = wq[:, h * 96 + 64:h * 96 + 96][:, perm]
        w_krsw[l] = np.asarray(inp["w_in"][l], np.float32)[:, 640:672][:, perm]
    shared = dict(
        cols=cols, fing=fing, cst=cst, sgb=sgb, sgln=sgln, sgwT=sgwT,
        w_in=f(inp["w_in"][:NL]), w_krsw=w_krsw, w_uq=f(inp["w_uq"][:NL]), w_uqsw=w_uqsw, w_ukv=f(inp["w_ukv"][:NL]),
        rw_w2=f(np.asarray(inp["rw_w2"][:NL]).reshape(NL, 128, 512)), rw_a2=f(np.asarray(inp["rw_a2"][:NL]).reshape(NL, 128, 512)),
        rw_g2=f(inp["rw_g2"][:NL]), w_branch=f(inp["w_branch"][:NL]), w_out=f(inp["w_out"][:NL]),
        w_ffn_gate=f(inp["w_ffn_gate"][:NL]), w_ffn_up=f(inp["w_ffn_up"][:NL]), w_ffn_down=f(inp["w_ffn_down"][:NL]),
    )
    return shared


def kernel(**inputs):
    NCORE, NSEQ, NL = 8, 4, 2
    x = np.asarray(inputs["x"], np.float32)
    pos = np.asarray(inputs["positions"], np.int32)
    shared = _host_layout(inputs, NL)
    key = (NSEQ, NL)
    if key not in _NC_CACHE:
        _NC_CACHE[key] = build(NSEQ, NL)
    nc = _NC_CACHE[key]
    in_maps = []
    for c in range(NCORE):
        m = dict(shared)
        m["x"] = np.ascontiguousarray(x[c * NSEQ:(c + 1) * NSEQ].reshape(NSEQ * S, D))
        m["pos"] = np.ascontiguousarray(pos[c * NSEQ:(c + 1) * NSEQ])
        in_maps.append(m)
    res = run_bass_kernel_spmd(nc, in_maps, core_ids=list(range(NCORE)))
    out = np.concatenate([np.asarray(r["out"]).reshape(NSEQ, S, D) for r in res.results], axis=0)
    return out.astype(np.float32)
```
